# Optimizing a Trainium2 kernel written in Bass

```python
import math
import jax, jax.numpy as jnp
from jax import lax
import numpy as np

D_MODEL = 1024
BATCH = 4
SEQ = 8192
DEPTH = 2

CHUNK = 64
HEAD_DIM = 64
ROPE_THETA = 10000.0
EPS = 1e-6
NEG_INF = -1e30
A_Q_HEADS = 8
A_KV_HEADS = 2
A_WINDOW = 128
A_WIN_CHUNKS = A_WINDOW // CHUNK
B_HEADS = 8
B_PREV_CHUNKS = 8
B_MAX_REL = 128
C_HEADS = 8
C_V_DIM = 2 * HEAD_DIM
Q_BLOCK = 128

A_Q = A_Q_HEADS * HEAD_DIM
A_KV = A_KV_HEADS * HEAD_DIM
A_W = A_Q
B_W = B_HEADS * HEAD_DIM
EVEN_IN = A_Q + 2 * A_KV + A_W + 4 * B_W
EVEN_MIX = A_W + B_W
C_QK = C_HEADS * 2 * HEAD_DIM
C_W = C_HEADS * C_V_DIM
ODD_IN = 2 * C_QK + 2 * C_W
N_EVEN = (DEPTH + 1) // 2
N_ODD = DEPTH // 2
EVEN_SIZES = (A_Q, A_KV, A_KV, A_W, B_W, B_W, B_W, B_W)

kernel_name = "chunk_causal_hybrid_swa_relbias_diffattn"


def rms_norm(x, g):
    xf = x.astype(jnp.float32)
    y = xf * lax.rsqrt(jnp.mean(xf * xf, axis=-1, keepdims=True) + EPS)
    return (y * g.astype(jnp.float32)).astype(x.dtype)


def rope_tables(seq):
    inv = 1.0 / (ROPE_THETA ** (jnp.arange(0, HEAD_DIM, 2, dtype=jnp.float32) / HEAD_DIM))
    ang = jnp.arange(seq, dtype=jnp.float32)[:, None] * inv[None, :]
    return jnp.cos(ang), jnp.sin(ang)


def apply_rope(x, cos, sin):
    x1, x2 = jnp.split(x.astype(jnp.float32), 2, axis=-1)
    c = cos[None, :, None, :]
    s = sin[None, :, None, :]
    return jnp.concatenate([x1 * c - x2 * s, x2 * c + x1 * s], axis=-1).astype(x.dtype)


def to_chunks(t):
    b, s = t.shape[0], t.shape[1]
    return t.reshape(b, s // CHUNK, CHUNK, *t.shape[2:])


def chunk_band(t, n_prev):
    nc = t.shape[1]
    tp = jnp.pad(t, ((0, 0), (n_prev, 0), (0, 0), (0, 0), (0, 0)))
    band = jnp.stack([tp[:, j:j + nc] for j in range(n_prev + 1)], axis=2)
    return band.reshape(t.shape[0], nc, (n_prev + 1) * CHUNK, *t.shape[3:])


def band_valid(nc, n_prev):
    c = jnp.arange(nc)[:, None]
    j = jnp.arange(n_prev + 1)[None, :]
    return jnp.repeat((c - n_prev + j) >= 0, CHUNK, axis=1)


def sliding_window_sink_attention(q, k, v, sinks):
    b, s = q.shape[0], q.shape[1]
    nc = s // CHUNK
    g = A_Q_HEADS // A_KV_HEADS
    qc = to_chunks(q).reshape(b, nc, CHUNK, A_KV_HEADS, g, HEAD_DIM)
    kb = chunk_band(to_chunks(k), A_WIN_CHUNKS)
    vb = chunk_band(to_chunks(v), A_WIN_CHUNKS)
    scores = jnp.einsum('bcqhgd,bclhd->bchgql', qc, kb).astype(jnp.float32) / math.sqrt(HEAD_DIM)
    valid = band_valid(nc, A_WIN_CHUNKS)[None, :, None, None, None, :]
    scores = jnp.where(valid, scores, NEG_INF)
    sink = sinks.astype(jnp.float32).reshape(A_KV_HEADS, g)[None, None, :, :, None, None]
    sink = jnp.broadcast_to(sink, scores.shape[:-1] + (1,))
    probs = jax.nn.softmax(jnp.concatenate([scores, sink], axis=-1), axis=-1)[..., :-1]
    out = jnp.einsum('bchgql,bclhd->bcqhgd', probs.astype(v.dtype), vb)
    return out.reshape(b, s, A_Q_HEADS * HEAD_DIM)


def chunked_relbias_attention(q, k, v, rel_table):
    b, s = q.shape[0], q.shape[1]
    nc = s // CHUNK
    band_len = (B_PREV_CHUNKS + 1) * CHUNK
    qc = to_chunks(q)
    kb = chunk_band(to_chunks(k), B_PREV_CHUNKS)
    vb = chunk_band(to_chunks(v), B_PREV_CHUNKS)
    scores = jnp.einsum('bcqhd,bclhd->bhcql', qc, kb).astype(jnp.float32) / math.sqrt(HEAD_DIM)
    qpos = B_PREV_CHUNKS * CHUNK + jnp.arange(CHUNK)
    kpos = jnp.arange(band_len)
    rel = jnp.clip(qpos[:, None] - kpos[None, :], -B_MAX_REL, B_MAX_REL) + B_MAX_REL
    bias = rel_table.astype(jnp.float32)[:, rel]
    scores = scores + bias[None, :, None, :, :]
    valid = band_valid(nc, B_PREV_CHUNKS)[None, None, :, None, :]
    probs = jax.nn.softmax(jnp.where(valid, scores, NEG_INF), axis=-1)
    out = jnp.einsum('bhcql,bclhd->bcqhd', probs.astype(v.dtype), vb)
    return out.reshape(b, s, B_HEADS * HEAD_DIM)


def differential_attention(q1, q2, k1, k2, v, lam):
    b, s = q1.shape[0], q1.shape[1]
    nb = s // Q_BLOCK
    scale = 1.0 / math.sqrt(HEAD_DIM)
    kchunk = jnp.arange(s) // CHUNK

    def block(i):
        start = i * Q_BLOCK
        qb1 = lax.dynamic_slice_in_dim(q1, start, Q_BLOCK, axis=1)
        qb2 = lax.dynamic_slice_in_dim(q2, start, Q_BLOCK, axis=1)
        qchunk = (start + jnp.arange(Q_BLOCK)) // CHUNK
        mask = (kchunk[None, :] <= qchunk[:, None])[None, None]
        s1 = jnp.einsum('bqhd,bkhd->bhqk', qb1, k1).astype(jnp.float32) * scale
        s2 = jnp.einsum('bqhd,bkhd->bhqk', qb2, k2).astype(jnp.float32) * scale
        p1 = jax.nn.softmax(jnp.where(mask, s1, NEG_INF), axis=-1)
        p2 = jax.nn.softmax(jnp.where(mask, s2, NEG_INF), axis=-1)
        a = p1 - lam * p2
        return jnp.einsum('bhqk,bkhd->bqhd', a.astype(v.dtype), v)

    out = lax.map(block, jnp.arange(nb))
    return jnp.transpose(out, (1, 0, 2, 3, 4)).reshape(b, s, C_HEADS, C_V_DIM)


def even_layer(x, norm_g, w_in, w_out, a_qn, a_kn, a_sinks, b_qn, b_kn, b_rel, cos, sin):
    b, s, _ = x.shape
    h = rms_norm(x, norm_g)
    proj = h @ w_in
    cuts = [int(c) for c in np.cumsum(EVEN_SIZES)[:-1]]
    aq, ak, av, ag, bq, bk, bv, bg = jnp.split(proj, cuts, axis=-1)
    aq = apply_rope(rms_norm(aq.reshape(b, s, A_Q_HEADS, HEAD_DIM), a_qn), cos, sin)
    ak = apply_rope(rms_norm(ak.reshape(b, s, A_KV_HEADS, HEAD_DIM), a_kn), cos, sin)
    av = av.reshape(b, s, A_KV_HEADS, HEAD_DIM)
    ya = sliding_window_sink_attention(aq, ak, av, a_sinks) * jax.nn.silu(ag)
    bq = rms_norm(bq.reshape(b, s, B_HEADS, HEAD_DIM), b_qn)
    bk = rms_norm(bk.reshape(b, s, B_HEADS, HEAD_DIM), b_kn)
    bv = bv.reshape(b, s, B_HEADS, HEAD_DIM)
    yb = chunked_relbias_attention(bq, bk, bv, b_rel) * jax.nn.silu(bg)
    y = jnp.concatenate([ya, yb], axis=-1) @ w_out
    return x + y


def odd_layer(x, norm_g, w_in, w_out, qn, kn, lq1, lk1, lq2, lk2, subln_g, cos, sin, lambda_init):
    b, s, _ = x.shape
    h = rms_norm(x, norm_g)
    proj = h @ w_in
    q, k, v, gate = jnp.split(proj, [C_QK, 2 * C_QK, 2 * C_QK + C_W], axis=-1)
    q = rms_norm(q.reshape(b, s, C_HEADS, 2, HEAD_DIM), qn)
    k = rms_norm(k.reshape(b, s, C_HEADS, 2, HEAD_DIM), kn)
    q1 = apply_rope(q[:, :, :, 0], cos, sin)
    q2 = apply_rope(q[:, :, :, 1], cos, sin)
    k1 = apply_rope(k[:, :, :, 0], cos, sin)
    k2 = apply_rope(k[:, :, :, 1], cos, sin)
    v = v.reshape(b, s, C_HEADS, C_V_DIM)
    lam = (jnp.exp(jnp.sum(lq1.astype(jnp.float32) * lk1.astype(jnp.float32)))
           - jnp.exp(jnp.sum(lq2.astype(jnp.float32) * lk2.astype(jnp.float32)))
           + lambda_init)
    o = differential_attention(q1, q2, k1, k2, v, lam)
    o = rms_norm(o, subln_g) * (1.0 - lambda_init)
    y = (o.reshape(b, s, C_W) * jax.nn.silu(gate)) @ w_out
    return x + y


def setup_inputs(seed: int = 0) -> dict:
    key = jax.random.key(seed)
    ks = jax.random.split(key, 24)
    f32 = jnp.float32

    def gain(k, shape):
        return jnp.ones(shape, f32) + 0.02 * jax.random.normal(k, shape, f32)

    return {
        "x": jax.random.normal(ks[0], (BATCH, SEQ, D_MODEL), f32),
        "ev_norm": gain(ks[1], (N_EVEN, D_MODEL)),
        "ev_w_in": jax.random.normal(ks[2], (N_EVEN, D_MODEL, EVEN_IN), f32) * D_MODEL ** -0.5,
        "ev_w_out": jax.random.normal(ks[3], (N_EVEN, EVEN_MIX, D_MODEL), f32) * EVEN_MIX ** -0.5,
        "ev_a_q_norm": gain(ks[4], (N_EVEN, HEAD_DIM)),
        "ev_a_k_norm": gain(ks[5], (N_EVEN, HEAD_DIM)),
        "ev_a_sinks": jax.random.normal(ks[6], (N_EVEN, A_Q_HEADS), f32),
        "ev_b_q_norm": gain(ks[7], (N_EVEN, HEAD_DIM)),
        "ev_b_k_norm": gain(ks[8], (N_EVEN, HEAD_DIM)),
        "ev_b_rel_bias": 0.5 * jax.random.normal(ks[9], (N_EVEN, B_HEADS, 2 * B_MAX_REL + 1), f32),
        "od_norm": gain(ks[10], (N_ODD, D_MODEL)),
        "od_w_in": jax.random.normal(ks[11], (N_ODD, D_MODEL, ODD_IN), f32) * D_MODEL ** -0.5,
        "od_w_out": jax.random.normal(ks[12], (N_ODD, C_W, D_MODEL), f32) * C_W ** -0.5,
        "od_q_norm": gain(ks[13], (N_ODD, HEAD_DIM)),
        "od_k_norm": gain(ks[14], (N_ODD, HEAD_DIM)),
        "od_lambda_q1": 0.1 * jax.random.normal(ks[15], (N_ODD, HEAD_DIM), f32),
        "od_lambda_k1": 0.1 * jax.random.normal(ks[16], (N_ODD, HEAD_DIM), f32),
        "od_lambda_q2": 0.1 * jax.random.normal(ks[17], (N_ODD, HEAD_DIM), f32),
        "od_lambda_k2": 0.1 * jax.random.normal(ks[18], (N_ODD, HEAD_DIM), f32),
        "od_subln": gain(ks[19], (N_ODD, C_V_DIM)),
    }


def reference(x, ev_norm, ev_w_in, ev_w_out, ev_a_q_norm, ev_a_k_norm, ev_a_sinks,
              ev_b_q_norm, ev_b_k_norm, ev_b_rel_bias, od_norm, od_w_in, od_w_out,
              od_q_norm, od_k_norm, od_lambda_q1, od_lambda_k1, od_lambda_q2,
              od_lambda_k2, od_subln):
    cos, sin = rope_tables(x.shape[1])
    for layer in range(DEPTH):
        i = layer // 2
        if layer % 2 == 0:
            x = even_layer(x, ev_norm[i], ev_w_in[i], ev_w_out[i], ev_a_q_norm[i],
                           ev_a_k_norm[i], ev_a_sinks[i], ev_b_q_norm[i], ev_b_k_norm[i],
                           ev_b_rel_bias[i], cos, sin)
        else:
            lambda_init = 0.8 - 0.6 * math.exp(-0.3 * layer)
            x = odd_layer(x, od_norm[i], od_w_in[i], od_w_out[i], od_q_norm[i], od_k_norm[i],
                          od_lambda_q1[i], od_lambda_k1[i], od_lambda_q2[i], od_lambda_k2[i],
                          od_subln[i], cos, sin, lambda_init)
    return x
```

```python
import math
from contextlib import ExitStack
import numpy as np
import ml_dtypes
import concourse.bass as bass
import concourse.mybir as mybir
from concourse.bass_utils import run_bass_kernel_spmd

F32 = mybir.dt.float32
BF16 = mybir.dt.bfloat16
AF = mybir.ActivationFunctionType
ALU = mybir.AluOpType
AX = mybir.AxisListType
NPBF = ml_dtypes.bfloat16
COMPUTE = ("pe", "act", "dve", "pool")
import os as _os0
_SERIAL = int(_os0.environ.get("BASS_SERIAL", "0"))
_SER_ON = [_SERIAL == 1]
_CHAIN = int(_os0.environ.get("BASS_CHAIN", "0"))
EPS = 1e-6
NEG = -30000.0


class Prog:
    def __init__(self, nc, stack, prefix="", semstack=None, shared=None):
        self.nc = nc
        self.stack = semstack if semstack is not None else stack
        self.prefix = prefix
        self.shared = shared
        self.ops = []
        self.lastw = {}
        self.readers = {}
        self.excl_state = {}
        self._last_on = {}

    def op(self, eng, fn, reads=(), writes=(), dma=False, excl=(), cc=False):
        i = len(self.ops)
        deps = set()
        if _SER_ON[0] and i > 0:
            deps.add((i - 1, "raw"))
        if _SERIAL == 4 or (_SERIAL == 5 and eng == "sp") or (_SERIAL == 6 and eng == "pool") or (_SERIAL == 7 and eng in ("act", "dve")) or (_CHAIN and (not dma) and eng in ("act", "dve", "pool")):
            lk = eng if _SERIAL >= 4 else (eng, "c")
            pe_ = self._last_on.get(lk)
            if pe_ is not None:
                deps.add((pe_, "raw"))
            self._last_on[lk] = i
        for k in excl:
            st = self.excl_state.get(k)
            if st is None:
                self.excl_state[k] = [eng, i, None]
            elif st[0] == eng:
                st[1] = i
                if st[2] is not None:
                    deps.add((st[2], "raw"))
            else:
                deps.add((st[1], "raw"))
                self.excl_state[k] = [eng, i, st[1]]
        for k in reads:
            w = self.lastw.get(k)
            if w is not None:
                deps.add((w, "raw"))
        for k in writes:
            w = self.lastw.get(k)
            if w is not None:
                deps.add((w, "waw"))
            for r in self.readers.get(k, ()):
                if r != i:
                    deps.add((r, "war"))
        for k in reads:
            self.readers.setdefault(k, []).append(i)
        for k in writes:
            self.lastw[k] = i
            self.readers[k] = []
        self.ops.append(dict(eng=eng, fn=fn, deps=deps, dma=dma, sig=False, cc=cc,
                             wkey=(writes[0] if (dma and writes) else None)))
        return i

    def emit(self):
        ops = self.ops
        need = []
        for i, o in enumerate(ops):
            if o["dma"]:
                o["sig"] = True
            nl = []
            for (d, kind) in o["deps"]:
                p = ops[d]
                if (not p["dma"]) and (not o["dma"]) and p["eng"] == o["eng"]:
                    if o["eng"] == "pe":
                        continue
                nl.append(d)
                p["sig"] = True
            need.append(nl)
        nc = self.nc
        sh = self.shared if self.shared is not None else {}
        if "eng" not in sh:
            sh["eng"] = {e: self.stack.enter_context(nc.semaphore(self.prefix + "s_" + e)) for e in COMPUTE}
            sh["engcnt"] = {e: 0 for e in COMPUTE}
            sh["pool"] = []
            sh["poolcnt"] = []
        sem_eng = sh["eng"]
        cnt = sh["engcnt"]
        pool, poolcnt = sh["pool"], sh["poolcnt"]
        dma_idx = {}
        waited = {}
        sigval = {}
        nw = 0
        per_eng = {e: [] for e in ("pe", "act", "dve", "pool", "sp")}
        for i, o in enumerate(ops):
            e = o["eng"]
            reqs = {}
            for d in need[i]:
                s, v = sigval[d]
                key = id(s)
                if key not in reqs or reqs[key][1] < v:
                    reqs[key] = (s, v)
            wl = []
            for key, (s, v) in reqs.items():
                if waited.get((e, key), -1) >= v:
                    continue
                wl.append((s, v))
                nw += 1
                waited[(e, key)] = v
            sg = None
            if o["sig"]:
                if o["dma"]:
                    wk = o["wkey"]
                    if wk not in dma_idx:
                        k = len(dma_idx)
                        if k >= len(pool):
                            pool.append(self.stack.enter_context(nc.semaphore(self.prefix + "sd%d" % k)))
                            poolcnt.append(0)
                        dma_idx[wk] = k
                    k = dma_idx[wk]
                    inc = 1 if o.get("cc") else 16
                    poolcnt[k] += inc
                    sg = (pool[k], inc)
                    sigval[i] = (pool[k], poolcnt[k])
                else:
                    cnt[e] += 1
                    sg = (sem_eng[e], 1)
                    sigval[i] = (sem_eng[e], cnt[e])
            per_eng[e].append((wl, o["fn"], sg))

        def runner(lst):
            def f(engine):
                for (wl, fn, sg) in lst:
                    for (s, v) in wl:
                        engine.wait_ge(s, v)
                    ins = fn()
                    if sg is not None:
                        ins.then_inc(sg[0], sg[1])
            return f

        with nc.Block() as block:
            bmap = {"pe": block.tensor, "act": block.scalar, "dve": block.vector,
                    "pool": block.gpsimd, "sp": block.sync}
            for e in ("sp", "act", "dve", "pool", "pe"):
                if per_eng[e]:
                    bmap[e](runner(per_eng[e]))
        return dict(n_ops=len(ops), n_waits=nw, n_dma_sems=len(dma_idx), cnt=dict(cnt))


class Ctx:
    def __init__(self, nc, st, nf32=7, prefix="", semstack=None, shared=None):
        self.nc = nc
        self.st = st
        self.prefix = prefix
        self.P = Prog(nc, st, prefix=prefix, semstack=semstack, shared=shared)
        self.banks = [st.enter_context(nc.psum_tensor(prefix + "bank%d" % i, [128, 512], F32)) for i in range(nf32)]
        if nf32 < 8:
            self.bankb = st.enter_context(nc.psum_tensor(prefix + "bankb", [128, 1024], BF16))

    def sb(self, name, shape, dt):
        return self.st.enter_context(self.nc.sbuf_tensor(self.prefix + name, shape, dt))

    def dmaf(self, fn, reads, writes, eng="sp"):
        self.P.op(eng, fn, reads=reads, writes=writes, dma=True)

    def barrier(self, keys):
        nc = self.nc
        for e, engine in (("sp", nc.sync), ("act", nc.scalar), ("dve", nc.vector), ("pool", nc.gpsimd), ("pe", nc.tensor)):
            self.P.op(e, (lambda en: (lambda: en.nop()))(engine), reads=list(keys))

    def dma(self, out, in_, reads, writes, eng="sp"):
        nc = self.nc
        q = nc.sync if eng == "sp" else nc.gpsimd
        self.P.op(eng, lambda: q.dma_start(out=out, in_=in_), reads=reads, writes=writes, dma=True)

    def mm(self, out, lhsT, rhs, start, stop, reads, writes, bank):
        nc = self.nc
        self.P.op("pe", lambda: nc.tensor.matmul(out, lhsT=lhsT, rhs=rhs, start=start, stop=stop,
                                                 skip_group_check=True),
                  reads=reads, writes=writes, excl=[("bank", bank)])

    def tr(self, out, in_, ident, reads, writes, bank):
        nc = self.nc
        self.P.op("pe", lambda: nc.tensor.transpose(out=out, in_=in_, identity=ident),
                  reads=reads, writes=writes, excl=[("bank", bank)])

    def act(self, out, in_, func, reads, writes, banks=(), scale=None, bias=None, accum_out=None):
        nc = self.nc
        kw = {}
        if scale is not None:
            kw["scale"] = scale
        if bias is not None:
            kw["bias"] = bias
        if accum_out is not None:
            kw["accum_out"] = accum_out
        self.P.op("act", lambda: nc.scalar.activation(out=out, in_=in_, func=func, **kw),
                  reads=reads, writes=writes, excl=[("bank", b) for b in banks])

    def ve(self, eng, name, reads, writes, banks=(), **kw):
        nc = self.nc
        e = nc.vector if eng == "dve" else nc.gpsimd
        f = getattr(e, name)
        self.P.op(eng, lambda: f(**kw), reads=reads, writes=writes, excl=[("bank", b) for b in banks])


def rmsnorm_tile(C, xt, hb, sq, ss, kx, kh, tag):
    C.act(sq, xt, AF.Square, reads=[kx], writes=[("sq", tag)], accum_out=ss)
    C.ve("dve", "tensor_scalar", reads=[("sq", tag)], writes=[("ss1", tag)],
         out=ss, in0=ss, scalar1=1.0 / 1024, scalar2=EPS, op0=ALU.mult, op1=ALU.add)
    C.act(ss, ss, AF.Ln, reads=[("ss1", tag)], writes=[("ss2", tag)])
    C.act(ss, ss, AF.Exp, reads=[("ss2", tag)], writes=[("ss3", tag)], scale=-0.5)
    C.ve("dve", "tensor_scalar", reads=[("ss3", tag), kx], writes=[kh],
         out=hb, in0=xt, scalar1=ss, scalar2=None, op0=ALU.mult)


def transpose8(C, dst, src, ident, ksrc, kdst, bank, eng="act"):
    assert bank == 7
    pb = C.bankb
    for c in range(8):
        C.tr(pb[:, c * 128:(c + 1) * 128], src[:, c * 128:(c + 1) * 128], ident,
             reads=[ksrc], writes=[("pstr", bank)], bank=bank)
    src3 = pb.rearrange("p (c t) -> p c t", c=8)
    if eng == "act":
        C.act(dst, src3, AF.Copy, reads=[("pstr", bank)], writes=[kdst], banks=[bank])
    else:
        C.ve("dve", "tensor_copy", reads=[("pstr", bank)], writes=[kdst], banks=[bank], out=dst, in_=src3)


def load_weights(C, wdram, wsb, ncols, gcols, stage, tag, piece):
    npieces = (ncols + piece - 1) // piece
    i = 0
    for c in range(8):
        for pc in range(npieces):
            lo = pc * piece
            hi = min(ncols, lo + piece)
            sl = i % 2
            C.dma(stage[sl][:, 0:hi - lo], wdram[c * 128:(c + 1) * 128, lo:hi], reads=[], writes=[("wst", sl)])
            if gcols is not None:
                C.ve("dve", "tensor_scalar", reads=[("wst", sl), "cols"], writes=[("w", tag, c, pc)],
                     out=wsb[:, c, lo:hi], in0=stage[sl][:, 0:hi - lo], scalar1=gcols[:, c:c + 1],
                     scalar2=None, op0=ALU.mult)
            else:
                C.ve("pool", "tensor_copy", reads=[("wst", sl)], writes=[("w", tag, c, pc)],
                     out=wsb[:, c, lo:hi], in_=stage[sl][:, 0:hi - lo])
            i += 1
    return [("w", tag, c, pc) for c in range(8) for pc in range(npieces)]


def attn_unit(C, sbank, bias_rhs, qk_list, et, et_key, av_list, rd_extra):
    S = C.banks[sbank]
    first = True
    for (rhs, key) in bias_rhs:
        C.mm(S[:, :], C.ident, rhs, start=first, stop=False, reads=[key, "consts"],
             writes=[("S", sbank)], bank=sbank)
        first = False
    nq = len(qk_list)
    for j, (oap, kT, qT, rk) in enumerate(qk_list):
        C.mm(oap, kT, qT, start=first, stop=True, reads=list(rk),
             writes=[("S", sbank)], bank=sbank)
    C.act(et, S[:, :], AF.Exp, reads=[("S", sbank)] + list(rd_extra), writes=[et_key], banks=[sbank])
    for (o_ap, lo, hi, v_ap, rk, obank) in av_list:
        C.mm(o_ap, et[:, lo:hi], v_ap, start=False, stop=False, reads=[et_key] + list(rk),
             writes=[("O", obank)], bank=obank)


NT0 = 18
TM0 = NT0 * 128
W0C = TM0 + 1664


def build_l0(F=None):
    nc = F["nc"] if F else bass.Bass("TRN2", target_bir_lowering=False)
    dt = nc.dram_tensor
    xl = dt("xl", [4608, 1024], F32, kind="ExternalInput").ap()
    w0 = dt("w0", [1024, W0C], F32, kind="ExternalInput").ap()
    wo0 = dt("wo0", [1024, 1024], F32, kind="ExternalInput").ap()
    cols0 = dt("cols0", [128, 16], F32, kind="ExternalInput").ap()
    sinksb = dt("sinksb", [128, 8], F32, kind="ExternalInput").ap()
    biasB = dt("biasB", [128, 8, 512], F32, kind="ExternalInput").ap()
    maskA = dt("maskA", [128, 2, 512], BF16, kind="ExternalInput").ap()
    rope0 = dt("rope0", [2, 128, 4608], F32, kind="ExternalInput").ap()
    cst = dt("cst", [128, 256], BF16, kind="ExternalInput").ap()
    if F:
        x1o = F["x1s"].ap()
        h1To = F["h1src"].ap()
    else:
        x1o = dt("x1o", [4096, 1024], F32, kind="ExternalOutput").ap()
        h1To = dt("h1To", [8, 128, 4096], BF16, kind="ExternalOutput").ap()
    with ExitStack() as st:
        C = Ctx(nc, st, prefix="a_" if F else "", semstack=F["semstack"] if F else None, shared=F["shared"] if F else None)
        if F:
            fused_load_regs(C, F)
        sb = C.sb
        W = sb("W0b", [128, 8, W0C], BF16)
        Wo = sb("Wo", [128, 8, 1024], BF16)
        cols = sb("cols", [128, 16], F32)
        sinks = sb("sinks", [128, 8], F32)
        bBf = [sb("bBf%d" % i, [128, 512], F32) for i in range(2)]
        bBh = sb("bBh", [128, 8, 512], BF16)
        bBl = sb("bBl", [128, 8, 512], BF16)
        mA = sb("mA", [128, 2, 512], BF16)
        cs = sb("cs", [128, 256], BF16)
        epsc = sb("epsc", [128, 1], F32)
        C.ident = cs[:, 0:128]
        onesblk = cs[:, 128:256]
        stage = [sb("stg%d" % i, [128, 496], F32) for i in range(2)]
        KTA = sb("KTA", [128, 3, 512], BF16)
        KTB = sb("KTB", [128, 3, 4, 512], BF16)
        VA = sb("VA", [128, 12, 2, 65], BF16)
        VB = sb("VB", [128, 12, 8, 65], BF16)
        QTA = sb("QTA", [128, 4, 512], BF16)
        QTB = sb("QTB", [128, 4, 512], BF16)
        GT = sb("GT", [128, 4, 1024], BF16)
        hT = sb("hT", [128, 8, 512], BF16)
        xt = [sb("xt%d" % i, [128, 1024], F32) for i in range(2)]
        xr = [sb("xr%d" % i, [128, 1024], F32) for i in range(2)]
        hb = [sb("hb%d" % i, [128, 1024], BF16) for i in range(2)]
        sq = sb("sq", [128, 1024], BF16)
        ssb = [sb("ss%d" % i, [128, 1], F32) for i in range(4)]
        sqT = sb("sqT", [128, 512], BF16)
        rstd = sb("rstd", [128, 512], F32)
        t1 = sb("t1", [128, 512], F32)
        t2 = sb("t2", [128, 512], F32)
        tb = sb("tb", [128, 512], F32)
        ropeC = [sb("ropeC%d" % i, [128, 512], F32) for i in range(1)]
        ropeS = [sb("ropeS%d" % i, [128, 512], F32) for i in range(1)]
        ET = [sb("ET%d" % i, [128, 512], BF16) for i in range(3)]
        yb = sb("yb", [128, 1024], BF16)
        yf = sb("yf", [128, 1024], F32)
        yT = sb("yT", [128, 8, 128], BF16)
        h1T = [sb("h1T%d" % i, [128, 8, 128], BF16) for i in range(1)]
        rc = sb("rc", [128, 16], F32)
        esink = sb("esink", [128, 8], F32)

        C.dma(cols[:], cols0[:, :], [], ["cols_raw"])
        C.dma(sinks[:], sinksb[:, :], [], ["sinks_raw"])
        C.dma(cs[:], cst[:, :], [], ["consts"])
        C.dma(mA[:], maskA[:, :, :], [], ["mA"])
        C.ve("dve", "memset", [], ["epsc"], ap=epsc[:], constant=EPS)
        for cc in (0, 1, 4):
            C.ve("dve", "tensor_scalar", ["cols_raw"], ["cols_raw"], out=cols[:, cc:cc + 1], in0=cols[:, cc:cc + 1],
                 scalar1=0.125, scalar2=None, op0=ALU.mult)
        C.ve("dve", "tensor_copy", ["cols_raw"], ["cols"], out=rc[:, 0:1], in_=cols[:, 0:1])
        C.act(esink[:], sinks[:], AF.Exp, reads=["sinks_raw"], writes=["esink"])
        for j in range(8):
            bf = bBf[j % 2]
            C.dma(bf[:], biasB[:, j, :], [], [("bBf", j % 2)])
            C.ve("pool", "tensor_copy", [("bBf", j % 2)], ["bBh"], out=bBh[:, j, :], in_=bf[:])
            C.ve("pool", "tensor_copy", ["bBh"], ["tb"], out=tb[:], in_=bBh[:, j, :])
            C.ve("pool", "tensor_tensor", ["tb", ("bBf", j % 2)], ["tb"], out=tb[:], in0=bf[:], in1=tb[:], op=ALU.subtract)
            C.ve("pool", "tensor_copy", ["tb"], ["bBl"], out=bBl[:, j, :], in_=tb[:])
        C.ve("pool", "memset", [], ["VAinit"], ap=VA[:], constant=1.0)
        C.ve("pool", "memset", [], ["VBinit"], ap=VB[:], constant=1.0)
        wkeys = load_weights(C, w0, W, W0C, cols[:, 6:14], stage, "w0", 496)
        wokeys = load_weights(C, wo0, Wo, 1024, None, stage, "wo", 496)

        gcol = {"Aq": 0, "Aqs": 1, "Ak": 2, "Aks": 3, "Bq": 4, "Bk": 5}
        flag = cols[:, 14:15]

        for g in range(9):
            gs = g % 3
            if _SERIAL == 2:
                _SER_ON[0] = False
            if _SERIAL == 3:
                _SER_ON[0] = True
            for tt in range(4):
                t = g * 4 + tt
                b = t % 2
                C.dma(xt[b][:], xl[t * 128:(t + 1) * 128, :], [], [("xt", b)])
                rmsnorm_tile(C, xt[b][:], hb[b][:], sq[:], ssb[b][:], ("xt", b), ("hb", b), ("n", b))
                transpose8(C, hT[:, :, tt * 128:(tt + 1) * 128], hb[b][:], C.ident, ("hb", b), ("hT", tt), bank=7)
            hTk = [("hT", tt) for tt in range(4)]
            rb = 0
            C.dma(ropeC[rb][:], rope0[0, :, g * 512:(g + 1) * 512], [], [("ropeC", rb)])
            C.dma(ropeS[rb][:], rope0[1, :, g * 512:(g + 1) * 512], [], [("ropeS", rb)])

            def proj_T(tile, bank):
                for c in range(8):
                    C.mm(C.banks[bank][:, :], W[:, c, tile * 128:(tile + 1) * 128], hT[:, c, :],
                         start=(c == 0), stop=(c == 7), reads=hTk + wkeys, writes=[("pj", bank)], bank=bank)

            def rstd_from(bank):
                C.act(sqT[:], C.banks[bank][:, :], AF.Square, reads=[("pj", bank)], writes=["sqT"], banks=[bank])
                C.mm(C.banks[5][:, :], onesblk, sqT[:], start=True, stop=True, reads=["sqT", "consts"],
                     writes=[("pj", 5)], bank=5)
                C.act(rstd[:], C.banks[5][:, :], AF.Ln, reads=[("pj", 5), "epsc"], writes=["rstd0"], banks=[5],
                      scale=1.0 / 64, bias=epsc[:])
                C.act(rstd[:], rstd[:], AF.Exp, reads=["rstd0"], writes=["rstd"], scale=-0.5)

            def normrope(tile, stile, gq, gs_, dst, kdst):
                proj_T(tile, 3)
                proj_T(stile, 4)
                rstd_from(3)
                C.ve("dve", "scalar_tensor_tensor", [("pj", 3), "cols", ("ropeC", rb)], ["t1"], banks=[3],
                     out=t1[:], in0=C.banks[3][:, :], scalar=cols[:, gq:gq + 1], in1=ropeC[rb][:],
                     op0=ALU.mult, op1=ALU.mult)
                C.ve("dve", "scalar_tensor_tensor", [("pj", 4), "cols", ("ropeS", rb)], ["t2"], banks=[4],
                     out=t2[:], in0=C.banks[4][:, :], scalar=cols[:, gs_:gs_ + 1], in1=ropeS[rb][:],
                     op0=ALU.mult, op1=ALU.mult)
                C.ve("pool", "tensor_tensor", ["t1", "t2"], ["t1b"], out=t1[:], in0=t1[:], in1=t2[:], op=ALU.add)
                C.ve("dve", "tensor_tensor", ["t1b", "rstd"], [kdst], out=dst, in0=t1[:], in1=rstd[:], op=ALU.mult)

            def normonly(tile, gq, dst, kdst):
                proj_T(tile, 3)
                rstd_from(3)
                C.ve("dve", "scalar_tensor_tensor", [("pj", 3), "cols", "rstd"], [kdst], banks=[3],
                     out=dst, in0=C.banks[3][:, :], scalar=cols[:, gq:gq + 1], in1=rstd[:],
                     op0=ALU.mult, op1=ALU.mult)

            normrope(4, 9, gcol["Ak"], gcol["Aks"], KTA[:, gs, :], ("KTA", gs))
            for i in range(4):
                normonly(14 + i, gcol["Bk"], KTB[:, gs, i, :], ("KTB", gs, i))
            if g > 0:
                for i in range(4):
                    normrope(i, 5 + i, gcol["Aq"], gcol["Aqs"], QTA[:, i, :], ("QTA", i))
                for i in range(4):
                    normonly(10 + i, gcol["Bq"], QTB[:, i, :], ("QTB", i))

            for tt in range(4):
                vs = gs * 4 + tt
                segs = [(TM0, 128, "av"), (TM0 + 640, 512, "bv")]
                if g > 0:
                    segs += [(TM0 + 128, 512, "ag"), (TM0 + 1152, 512, "bg")]
                for (c0, n, kind) in segs:
                    bank = 3 if kind in ("av", "ag") else 4
                    for c in range(8):
                        C.mm(C.banks[bank][:, 0:n], hT[:, c, tt * 128:(tt + 1) * 128], W[:, c, c0:c0 + n],
                             start=(c == 0), stop=(c == 7), reads=[("hT", tt)] + wkeys, writes=[("pj", bank)], bank=bank)
                    src = C.banks[bank]
                    if kind == "av":
                        dst = VA[:, vs, :, 0:64]
                        s3 = src[:, 0:128].rearrange("p (h d) -> p h d", h=2)
                        rk, wk = ["VAinit"], [("VA", vs)]
                    elif kind == "bv":
                        dst = VB[:, vs, :, 0:64]
                        s3 = src[:, 0:512].rearrange("p (h d) -> p h d", h=8)
                        rk, wk = ["VBinit"], [("VB", vs)]
                    if kind in ("av", "bv"):
                        if g == 0:
                            C.ve("dve", "tensor_scalar", [("pj", bank), "cols"] + rk, wk, banks=[bank],
                                 out=dst, in0=s3, scalar1=flag, scalar2=None, op0=ALU.mult)
                            ones = (VA if kind == "av" else VB)[:, vs, :, 64:65]
                            C.ve("dve", "tensor_scalar", ["cols"] + wk, wk, out=ones, in0=ones, scalar1=flag,
                                 scalar2=None, op0=ALU.mult)
                        else:
                            C.ve("dve", "tensor_copy", [("pj", bank)] + rk, wk, banks=[bank], out=dst, in_=s3)
                            if g >= 3:
                                pass
                    else:
                        off = 0 if kind == "ag" else 512
                        C.act(GT[:, tt, off:off + 512], src[:, 0:512], AF.Silu, reads=[("pj", bank)],
                              writes=[("GT", tt, off)], banks=[bank])
                if g == 3:
                    C.ve("pool", "memset", [("VA", vs)], [("VA", vs)], ap=VA[:, vs, :, 64:65], constant=1.0)
                    C.ve("pool", "memset", [("VB", vs)], [("VB", vs)], ap=VB[:, vs, :, 64:65], constant=1.0)
            if g == 0:
                continue

            if _SERIAL == 2:
                _SER_ON[0] = True
            if _SERIAL == 3:
                _SER_ON[0] = False
            for tt in range(4):
                t = g * 4 + tt
                ot = t - 4
                xb = ot % 2
                C.dma(xr[xb][:], xl[t * 128:(t + 1) * 128, :], [], [("xr", xb), ("x1", xb, 0), ("x1", xb, 1)])
                def kslot(tk):
                    return (tk // 4) % 3, (tk % 4) * 128
                OA = [C.banks[0], C.banks[1]]
                C.ve("dve", "memset", [], [("O", 0)], banks=[0], ap=C.banks[0][:, :], constant=0.0)
                C.ve("dve", "memset", [], [("O", 1)], banks=[1], ap=C.banks[1][:, :], constant=0.0)
                ei = 0
                for kv in range(2):
                    pl = slice(kv * 64, kv * 64 + 64)
                    for bi, tk in enumerate((t - 1, t)):
                        ksl, ko = kslot(tk)
                        sbank = 3 + (ei % 2)
                        et = ET[ei % 3]
                        qk = [(C.banks[sbank][:, :].rearrange("p (s q) -> p s q", s=4), KTA[pl, ksl, ko:ko + 128],
                               QTA[pl, :, tt * 128:(tt + 1) * 128],
                               [("KTA", ksl)] + [("QTA", i) for i in range(4)])]
                        av = []
                        vsl = ksl * 4 + tk % 4
                        for s in range(4):
                            h = s + 4 * kv
                            av.append((OA[kv][:, s * 65:(s + 1) * 65], s * 128, (s + 1) * 128,
                                       VA[:, vsl, kv, :], [("VA", vsl)], kv))
                        attn_unit(C, sbank, [(mA[:, bi, :], "mA")], qk, et[:], ("ET", ei % 3), av, [])
                        ei += 1
                for kv in range(2):
                    o3 = OA[kv][:, 0:260].rearrange("p (s e) -> p s e", s=4)
                    C.ve("dve", "tensor_tensor", [("O", kv), "esink"], [("rcA", kv)], banks=[kv],
                         out=rc[:, kv * 4:kv * 4 + 4], in0=o3[:, :, 64], in1=esink[:, kv * 4:kv * 4 + 4], op=ALU.add)
                    C.ve("dve", "reciprocal", [("rcA", kv)], [("rcA2", kv)], out=rc[:, kv * 4:kv * 4 + 4],
                         in_=rc[:, kv * 4:kv * 4 + 4])
                    for s in range(4):
                        h = s + 4 * kv
                        C.ve("dve", "tensor_scalar", [("O", kv), ("rcA2", kv)], [("yfA", h)], banks=[kv],
                             out=yf[:, h * 64:(h + 1) * 64], in0=o3[:, s, 0:64], scalar1=rc[:, h:h + 1],
                             scalar2=None, op0=ALU.mult)
                C.ve("dve", "memset", [], [("O", 0)], banks=[0], ap=C.banks[0][:, :], constant=0.0)
                C.ve("dve", "memset", [], [("O", 1)], banks=[1], ap=C.banks[1][:, :], constant=0.0)
                btype = [0, 1, 1, 2, 3]
                for hg in range(2):
                    pl = slice(hg * 64, hg * 64 + 64)
                    for bi in range(5):
                        tk = t - 4 + bi
                        ksl, ko = kslot(tk)
                        vsl = ksl * 4 + tk % 4
                        sbank = 3 + (ei % 2)
                        et = ET[ei % 3]
                        bj = hg * 4 + btype[bi]
                        qk = []
                        av = []
                        for s in range(4):
                            h = 2 * s + hg
                            qk.append((C.banks[sbank][:, s * 128:(s + 1) * 128], KTB[pl, ksl, s, ko:ko + 128],
                                       QTB[pl, s, tt * 128:(tt + 1) * 128], [("KTB", ksl, s), ("QTB", s)]))
                            av.append((OA[hg][:, s * 65:(s + 1) * 65], s * 128, (s + 1) * 128,
                                       VB[:, vsl, h, :], [("VB", vsl)], hg))
                        attn_unit(C, sbank, [(bBh[:, bj, :], "bBh"), (bBl[:, bj, :], "bBl")], qk, et[:],
                                  ("ET", ei % 3), av, [])
                        ei += 1
                for hg in range(2):
                    o3 = OA[hg][:, 0:260].rearrange("p (s e) -> p s e", s=4)
                    C.ve("dve", "reciprocal", [("O", hg)], [("rcB", hg)], banks=[hg],
                         out=rc[:, 8 + hg * 4:8 + hg * 4 + 4], in_=o3[:, :, 64])
                    for s in range(4):
                        h = 2 * s + hg
                        C.ve("dve", "tensor_scalar", [("O", hg), ("rcB", hg)], [("yfB", h)], banks=[hg],
                             out=yf[:, 512 + h * 64:512 + (h + 1) * 64], in0=o3[:, s, 0:64],
                             scalar1=rc[:, 8 + hg * 4 + s:8 + hg * 4 + s + 1], scalar2=None, op0=ALU.mult)
                C.ve("pool", "tensor_tensor", [("yfA", h) for h in range(8)] + [("yfB", h) for h in range(8)]
                     + [("GT", tt, 0), ("GT", tt, 512)], ["yb"], out=yb[:], in0=yf[:], in1=GT[:, tt, :], op=ALU.mult)
                transpose8(C, yT[:], yb[:], C.ident, "yb", "yT", bank=7)
                for half in range(2):
                    bank = 3 + half
                    for c in range(8):
                        C.mm(C.banks[bank][:, :], yT[:, c, :], Wo[:, c, half * 512:(half + 1) * 512],
                             start=(c == 0), stop=(c == 7), reads=["yT"] + wokeys, writes=[("pj", bank)], bank=bank)
                    C.ve("dve", "tensor_tensor", [("pj", bank), ("xr", xb)], [("x1", xb, half)], banks=[bank],
                         out=xr[xb][:, half * 512:(half + 1) * 512], in0=C.banks[bank][:, :],
                         in1=xr[xb][:, half * 512:(half + 1) * 512], op=ALU.add)
                x1k = [("x1", xb, 0), ("x1", xb, 1)]
                C.dma(x1o[ot * 128:(ot + 1) * 128, :], xr[xb][:], x1k, [("x1o", xb)], eng="pool")
                hbi = 0
                C.act(sq[:], xr[xb][:], AF.Square, reads=x1k, writes=[("sq", "h1")], accum_out=ssb[2][:])
                C.ve("dve", "tensor_scalar", [("sq", "h1")], [("ss1", "h1")], out=ssb[2][:], in0=ssb[2][:],
                     scalar1=1.0 / 1024, scalar2=EPS, op0=ALU.mult, op1=ALU.add)
                C.act(ssb[2][:], ssb[2][:], AF.Ln, reads=[("ss1", "h1")], writes=[("ss2", "h1")])
                C.act(ssb[2][:], ssb[2][:], AF.Exp, reads=[("ss2", "h1")], writes=[("ss3", "h1")], scale=-0.5)
                C.ve("dve", "tensor_scalar", [("ss3", "h1")] + x1k, ["yb"], out=yb[:], in0=xr[xb][:],
                     scalar1=ssb[2][:], scalar2=None, op0=ALU.mult)
                transpose8(C, h1T[0][:], yb[:], C.ident, "yb", ("h1T", 0), bank=7)
                C.dma(h1To.rearrange("c p t -> p c t")[:, :, ot * 128:(ot + 1) * 128], h1T[0][:],
                      [("h1T", 0)], [("h1To", 0)], eng="pool")
        P = C.P
        if F:
            def cc1():
                return nc.gpsimd.collective_compute("AllGather", ALU.bypass, replica_groups=[list(range(8))],
                                                    ins=[F["h1src"].ap().rearrange("c p t -> (c p) t")],
                                                    outs=[F["h1all"].ap()])
            P.op("pool", cc1, reads=[("h1To", 0)], writes=["h1all"], dma=True, cc=True)
            C.barrier(["h1all", ("x1o", 0), ("x1o", 1)])
        else:
            P.op("sp", lambda: nc.sync.nop(), reads=[("x1o", 0), ("x1o", 1), ("h1To", 0)])
        print("L0", P.emit())
    return nc


def fused_load_regs(C, F):
    nc = C.nc
    it = C.sb("idxt", [1, 4], mybir.dt.int32)
    C.dma(it[:], F["idx"][:, :], [], ["idxt"])

    def ld():
        ins = None
        for j, nm in enumerate(("v0", "v1", "w0", "w1")):
            ins = nc.sync.reg_load(F["regs"][nm], it[0:1, j:j + 1])
        return ins
    C.P.op("sp", ld, reads=["idxt"], writes=[])


def fused_snap(C, F):
    nc = C.nc

    def sn():
        F["vals"]["v0"] = nc.sync.snap(F["regs"]["v0"], min_val=0, max_val=7)
        F["vals"]["v1"] = nc.sync.snap(F["regs"]["v1"], min_val=0, max_val=7)
        F["vals"]["w0"] = nc.sync.snap(F["regs"]["w0"], min_val=0, max_val=15)
        F["vals"]["w1"] = nc.sync.snap(F["regs"]["w1"], min_val=0, max_val=15)
        return nc.sync.nop()
    C.P.op("sp", sn, reads=[], writes=[])


def _swap_idx(base, nheads):
    idx = []
    for h in range(nheads):
        idx += list(range(base + h * 64 + 32, base + h * 64 + 64)) + list(range(base + h * 64, base + h * 64 + 32))
    return idx


def _rope_tables(pos):
    inv = 1.0 / (10000.0 ** (np.arange(0, 64, 2, dtype=np.float32) / 64.0))
    ang = pos.astype(np.float32)[None, :] * inv[:, None].astype(np.float32)
    c = np.cos(ang).astype(np.float32)
    s = np.sin(ang).astype(np.float32)
    C = np.concatenate([c, c, c, c], axis=0)
    S = np.concatenate([-s, s, -s, s], axis=0)
    return np.ascontiguousarray(np.stack([C, S], axis=0))


def _consts():
    cst = np.zeros((128, 256), dtype=np.float32)
    cst[:, 0:128] = np.eye(128)
    cst[0:64, 128:192] = 1.0
    cst[64:128, 192:256] = 1.0
    return cst.astype(NPBF)


def prep_l0(inp):
    x = np.asarray(inp["x"], dtype=np.float32)
    win = np.asarray(inp["ev_w_in"], dtype=np.float32)[0]
    aq0, ak0, av0, ag0, bq0, bk0, bv0, bg0 = 0, 512, 640, 768, 1280, 1792, 2304, 2816
    idx = []
    for t in range(4):
        idx += list(range(aq0 + t * 64, aq0 + (t + 1) * 64)) + list(range(aq0 + (t + 4) * 64, aq0 + (t + 5) * 64))
    idx += list(range(ak0, ak0 + 128))
    sw = _swap_idx(aq0, 8)
    for t in range(4):
        idx += sw[t * 64:(t + 1) * 64] + sw[(t + 4) * 64:(t + 5) * 64]
    idx += _swap_idx(ak0, 2)
    idx += list(range(bq0, bq0 + 512)) + list(range(bk0, bk0 + 512))
    idx += list(range(av0, av0 + 128)) + list(range(ag0, ag0 + 512)) + list(range(bv0, bv0 + 512)) + list(range(bg0, bg0 + 512))
    assert len(idx) == W0C
    w0 = np.ascontiguousarray(win[:, idx])
    wo0 = np.ascontiguousarray(np.asarray(inp["ev_w_out"], dtype=np.float32)[0])
    p64 = np.arange(128) % 64
    p64s = (p64 + 32) % 64
    aqn = np.asarray(inp["ev_a_q_norm"], np.float32)[0]
    akn = np.asarray(inp["ev_a_k_norm"], np.float32)[0]
    bqn = np.asarray(inp["ev_b_q_norm"], np.float32)[0]
    bkn = np.asarray(inp["ev_b_k_norm"], np.float32)[0]
    evn = np.asarray(inp["ev_norm"], np.float32)[0]
    cols = np.zeros((128, 16), np.float32)
    cols[:, 0] = aqn[p64]; cols[:, 1] = aqn[p64s]; cols[:, 2] = akn[p64]; cols[:, 3] = akn[p64s]
    cols[:, 4] = bqn[p64]; cols[:, 5] = bkn[p64]
    cols[:, 6:14] = evn.reshape(8, 128).T
    sinksb = np.ascontiguousarray(np.broadcast_to(np.asarray(inp["ev_a_sinks"], np.float32)[0][None, :], (128, 8)))
    tbl = np.asarray(inp["ev_b_rel_bias"], np.float32)[0]
    l = np.arange(128)[:, None]
    q = np.arange(128)[None, :]
    biasB = np.zeros((128, 8, 512), np.float32)
    for hg in range(2):
        for ty in range(4):
            if ty <= 1:
                relidx = np.full((128, 128), 256)
            elif ty == 2:
                relidx = np.clip(128 + q - l, -128, 128) + 128
            else:
                relidx = np.clip(q - l, -128, 128) + 128
            for s in range(4):
                h = 2 * s + hg
                tile_ = tbl[h][relidx]
                if ty == 0:
                    tile_ = np.where((l < 64) & (q >= 64), np.float32(NEG), tile_)
                if ty == 3:
                    tile_ = np.where((l >= 64) & (q < 64), np.float32(NEG), tile_)
                biasB[:, hg * 4 + ty, s * 128:(s + 1) * 128] = tile_
    mA = np.zeros((128, 2, 512), np.float32)
    m0 = np.where((l < 64) & (q >= 64), NEG, 0.0)
    m1 = np.where((l >= 64) & (q < 64), NEG, 0.0)
    for s in range(4):
        mA[:, 0, s * 128:(s + 1) * 128] = m0
        mA[:, 1, s * 128:(s + 1) * 128] = m1
    mA = mA.astype(NPBF)
    cst = _consts()
    maps = []
    for c in range(8):
        b, p = c // 2, c % 2
        xl = np.zeros((4608, 1024), np.float32)
        if p == 0:
            xl[512:] = x[b, 0:4096]
        else:
            xl[:] = x[b, 4096 - 512:8192]
        cc = cols.copy()
        cc[:, 14] = 0.0 if p == 0 else 1.0
        pos = np.arange(4608) + p * 4096 - 512
        maps.append(dict(xl=xl, w0=w0, wo0=wo0, cols0=cc, sinksb=sinksb, biasB=biasB, maskA=mA,
                         rope0=_rope_tables(pos), cst=cst))
    return maps


LAMBDA_INIT = 0.8 - 0.6 * math.exp(-0.3 * 1)


def build_l1(F=None):
    nc = F["nc"] if F else bass.Bass("TRN2", target_bir_lowering=False)
    dt = nc.dram_tensor
    h1T = None if F else dt("h1T", [8, 128, 8192], BF16, kind="ExternalInput").ap()
    w1 = dt("w1", [1024, 3072], F32, kind="ExternalInput").ap()
    cols1 = dt("cols1", [128, 16], F32, kind="ExternalInput").ap()
    lamv = dt("lamv", [128, 4, 64], F32, kind="ExternalInput").ap()
    rope1 = dt("rope1", [2, 128, 8192], F32, kind="ExternalInput").ap()
    maskC = dt("maskC", [128, 4, 512], BF16, kind="ExternalInput").ap()
    cst = dt("cst1", [128, 384], BF16, kind="ExternalInput").ap()
    ogTo = F["ogsrc"].ap() if F else dt("ogTo", [4, 128, 8192], BF16, kind="ExternalOutput").ap()
    with ExitStack() as st:
        C = Ctx(nc, st, nf32=8, prefix="b_" if F else "", semstack=F["semstack"] if F else None, shared=F["shared"] if F else None)
        if F:
            fused_snap(C, F)
            h1a3 = F["h1all"].ap().rearrange("(r n) t -> r n t", r=8)
            for j in range(2):
                for q4 in range(4):
                    def selh(j=j, q4=q4):
                        return nc.sync.dma_start(out=F["h1sel"].ap()[j:j + 1, q4 * 256:(q4 + 1) * 256, :],
                                                 in_=h1a3[bass.ds(F["vals"]["v%d" % j], 1), q4 * 256:(q4 + 1) * 256, :])
                    C.dmaf(selh, [], [("h1sel", j, q4)])
            h1sv = F["h1sel"].ap().rearrange("j (c p) t -> j p c t", c=8)
        sb = C.sb
        W = sb("W1b", [128, 8, 3072], BF16)
        cols = sb("cols", [128, 16], F32)
        lv = sb("lv", [128, 4, 64], F32)
        lt = sb("lt", [128, 64], F32)
        lam = sb("lam", [128, 4], F32)
        mC = sb("mC", [128, 4, 512], BF16)
        cs = sb("cs", [128, 384], BF16)
        epsc = sb("epsc", [128, 1], F32)
        C.ident = cs[:, 0:128]
        onesblk = cs[:, 128:256]
        allones = cs[:, 256:384]
        stage = [sb("stg%d" % i, [128, 512], F32) for i in range(2)]
        KT = sb("KT", [128, 2, 8192], BF16)
        VV = sb("VV", [128, 64, 2, 128], BF16)
        QT = sb("QT", [128, 2, 512], BF16)
        QZ = sb("QZ", [128, 2, 2, 512], BF16)
        GTt = sb("GTt", [128, 2, 512], BF16)
        hT = [sb("hT%d" % i, [128, 8, 512], BF16) for i in range(2)]
        ropeC = sb("ropeC", [128, 512], F32)
        ropeS = sb("ropeS", [128, 512], F32)
        sqT = sb("sqT", [128, 512], BF16)
        rstd = sb("rstd", [128, 512], F32)
        t1 = sb("t1", [128, 512], F32)
        t2 = sb("t2", [128, 512], F32)
        ET = [sb("ET%d" % i, [128, 512], BF16) for i in range(4)]
        o1 = sb("o1", [128, 512], F32)
        o2 = sb("o2", [128, 512], F32)
        r1 = sb("r1", [128, 512], F32)
        r2 = sb("r2", [128, 512], F32)
        ogt = [sb("ogt%d" % i, [128, 512], BF16) for i in range(2)]
        acc = [[sb("acc%d%d" % (i, c), [128, 512], F32) for c in range(2)] for i in range(2)]
        acch = [sb("acch%d" % c, [128, 512], BF16) for c in range(2)]
        accl = [sb("accl%d" % c, [128, 512], BF16) for c in range(2)]
        UC = [0]

        C.dma(cols[:], cols1[:, :], [], ["cols_raw"])
        C.dma(lv[:], lamv[:, :, :], [], ["lv"])
        C.dma(cs[:], cst[:, :], [], ["consts"])
        C.dma(mC[:], maskC[:, :, :], [], ["mC"])
        C.ve("dve", "memset", [], ["epsc"], ap=epsc[:], constant=EPS)
        C.ve("dve", "memset", [], ["lam0"], ap=lam[:], constant=0.0)
        C.ve("pool", "memset", [], ["QZinit"], ap=QZ[:], constant=0.0)
        for cc in (0, 1):
            C.ve("dve", "tensor_scalar", ["cols_raw"], ["cols_raw"], out=cols[:, cc:cc + 1], in0=cols[:, cc:cc + 1],
                 scalar1=0.125, scalar2=None, op0=ALU.mult)
        for i in range(2):
            C.ve("dve", "tensor_tensor", ["lv", "lam0"], ["lt"], out=lt[:], in0=lv[:, 2 * i, :], in1=lv[:, 2 * i + 1, :], op=ALU.mult)
            C.act(lt[:], lt[:], AF.Copy, reads=["lt"], writes=["lt", ("lam", i)], accum_out=lam[:, i:i + 1])
            C.act(lam[:, i:i + 1], lam[:, i:i + 1], AF.Exp, reads=[("lam", i)], writes=[("lame", i)])
        C.ve("dve", "tensor_tensor", [("lame", 0), ("lame", 1)], ["lamn0"], out=lam[:, 2:3], in0=lam[:, 1:2], in1=lam[:, 0:1], op=ALU.subtract)
        C.ve("dve", "tensor_scalar", ["lamn0"], ["lamn"], out=lam[:, 2:3], in0=lam[:, 2:3], scalar1=-LAMBDA_INIT, scalar2=None, op0=ALU.add)
        C.ve("dve", "tensor_scalar", ["cols_raw", "lam0"], ["cols"], out=lam[:, 3:4], in0=cols[:, 12:13], scalar1=1.0 - LAMBDA_INIT,
             scalar2=None, op0=ALU.mult)
        lamneg = lam[:, 2:3]
        sublnS = lam[:, 3:4]
        wkeys = load_weights(C, w1, W, 3072, cols[:, 4:12], stage, "w1", 512)
        h1v = None if F else h1T.rearrange("c p t -> p c t")

        import os as _os
        _NG = int(_os.environ.get("L1_NG", "16"))
        for hp in range(int(_os.environ.get("L1_NP", "2"))):
            wb = hp * 1536
            for g in range(_NG):
                hb_ = (hp * 16 + g) % 2
                hTg = hT[hb_]
                if F:
                    C.dma(hTg[:], h1sv[g // 8, :, :, (g % 8) * 512:(g % 8 + 1) * 512],
                          [("h1sel", g // 8, q4) for q4 in range(4)], [("hT", hb_)])
                else:
                    C.dma(hTg[:], h1v[:, :, g * 512:(g + 1) * 512], [], [("hT", hb_)])
                C.dma(ropeC[:], rope1[0, :, g * 512:(g + 1) * 512], [], ["ropeC"])
                C.dma(ropeS[:], rope1[1, :, g * 512:(g + 1) * 512], [], ["ropeS"])
                hk = [("hT", hb_)]

                def proj_T(tile, bank):
                    for c in range(8):
                        C.mm(C.banks[bank][:, :], W[:, c, wb + tile * 128:wb + (tile + 1) * 128], hTg[:, c, :],
                             start=(c == 0), stop=(c == 7), reads=hk + wkeys, writes=[("pj", bank)], bank=bank)

                def rstd_from(bank, ones_ap, inv_n):
                    C.act(sqT[:], C.banks[bank][:, :], AF.Square, reads=[("pj", bank)], writes=["sqT"], banks=[bank])
                    C.mm(C.banks[6][:, :], ones_ap, sqT[:], start=True, stop=True, reads=["sqT", "consts"],
                         writes=[("pj", 6)], bank=6)
                    C.act(rstd[:], C.banks[6][:, :], AF.Ln, reads=[("pj", 6), "epsc"], writes=["rstd0"], banks=[6],
                          scale=inv_n, bias=epsc[:])
                    C.act(rstd[:], rstd[:], AF.Exp, reads=["rstd0"], writes=["rstd"], scale=-0.5)

                def normrope(tile, stile, gq, gs_, dst, kdst):
                    proj_T(tile, 4)
                    proj_T(stile, 5)
                    rstd_from(4, onesblk, 1.0 / 64)
                    C.ve("dve", "scalar_tensor_tensor", [("pj", 4), "cols", "ropeC"], ["t1"], banks=[4],
                         out=t1[:], in0=C.banks[4][:, :], scalar=cols[:, gq:gq + 1], in1=ropeC[:],
                         op0=ALU.mult, op1=ALU.mult)
                    C.ve("dve", "scalar_tensor_tensor", [("pj", 5), "cols", "ropeS"], ["t2"], banks=[5],
                         out=t2[:], in0=C.banks[5][:, :], scalar=cols[:, gs_:gs_ + 1], in1=ropeS[:],
                         op0=ALU.mult, op1=ALU.mult)
                    C.ve("pool", "tensor_tensor", ["t1", "t2"], ["t1b"], out=t1[:], in0=t1[:], in1=t2[:], op=ALU.add)
                    C.ve("dve", "tensor_tensor", ["t1b", "rstd"], [kdst], out=dst, in0=t1[:], in1=rstd[:], op=ALU.mult)

                for hh in range(2):
                    normrope(2 + hh, 6 + hh, 2, 3, KT[:, hh, g * 512:(g + 1) * 512], ("KT", hh, g))
                    normrope(hh, 4 + hh, 0, 1, QT[:, hh, :], ("QT0", hh))
                    for c in range(2):
                        pl = slice(c * 64, c * 64 + 64)
                        C.ve("pool", "tensor_copy", [("QT0", hh), "QZinit"], [("QT", hh)] if c == 1 else [("QTa", hh)],
                             out=QZ[pl, hh, c, :], in_=QT[pl, hh, :])
                    proj_T(8 + hh, 4)
                    C.act(GTt[:, hh, :], C.banks[4][:, :], AF.Silu, reads=[("pj", 4)], writes=[("GTt", hh)], banks=[4])
                for tt in range(4):
                    vt = g * 4 + tt
                    for c in range(8):
                        C.mm(C.banks[7][:, 0:256], hTg[:, c, tt * 128:(tt + 1) * 128], W[:, c, wb + 1280:wb + 1536],
                             start=(c == 0), stop=(c == 7), reads=hk + wkeys, writes=[("pj", 7)], bank=7)
                    C.ve("dve", "tensor_copy", [("pj", 7)], [("VV", vt)], banks=[7],
                         out=VV[:, vt, :, :], in_=C.banks[7][:, 0:256].rearrange("p (h e) -> p h e", h=2))

                for hh in range(2):
                    ob0 = 2 * hh
                    for c in range(2):
                        C.ve("dve", "memset", [], [("O", ob0 + c)], banks=[ob0 + c], ap=C.banks[ob0 + c][:, :], constant=0.0)
                    nkb = 4 * g + 4
                    units = [(kb, c) for kb in range(nkb) for c in range(2)]
                    LA = 2
                    uc0 = UC[0]

                    def e_qk(i):
                        kb, c = units[i]
                        j = kb - 4 * g
                        pl = slice(c * 64, c * 64 + 64)
                        sbank = 4 + (uc0 + i) % 4
                        S = C.banks[sbank]
                        first = True
                        if j >= 0:
                            C.mm(S[:, :], C.ident, mC[:, j, :], start=True, stop=False, reads=["mC", "consts"],
                                 writes=[("S", sbank)], bank=sbank)
                            first = False
                        C.mm(S[:, :], KT[:, hh, kb * 128:(kb + 1) * 128], QZ[:, hh, c, :], start=first, stop=True,
                             reads=[("KT", hh, kb // 4), ("QT", hh), ("QTa", hh)], writes=[("S", sbank)], bank=sbank)

                    def e_exp_av(i):
                        kb, c = units[i]
                        sbank = 4 + (uc0 + i) % 4
                        ei = (uc0 + i) % 4
                        C.act(ET[ei][:], C.banks[sbank][:, :], AF.Exp, reads=[("S", sbank)], writes=[("ET", ei)], banks=[sbank])
                        C.mm(C.banks[ob0 + c][:, :], VV[:, kb, hh, :], ET[ei][:], start=False, stop=False,
                             reads=[("ET", ei), ("VV", kb)], writes=[("O", ob0 + c)], bank=ob0 + c)
                        eng = "pool" if c == 0 else "dve"
                        if kb == 0:
                            C.ve(eng, "tensor_copy", [("ET", ei)], [("acc", hh, c)], out=acc[hh][c][:], in_=ET[ei][:])
                        else:
                            C.ve(eng, "tensor_tensor", [("ET", ei), ("acc", hh, c)], [("acc", hh, c)], out=acc[hh][c][:],
                                 in0=acc[hh][c][:], in1=ET[ei][:], op=ALU.add)

                    for i in range(len(units) + LA):
                        if i < len(units):
                            e_qk(i)
                        if i - LA >= 0:
                            e_exp_av(i - LA)
                    UC[0] += len(units)
                    dbanks = []
                    for c in range(2):
                        C.ve("pool", "tensor_copy", [("acc", hh, c)], [("acch", c)], out=acch[c][:], in_=acc[hh][c][:])
                        C.ve("pool", "tensor_tensor", [("acc", hh, c), ("acch", c)], [("acc", hh, c)], out=acc[hh][c][:],
                             in0=acc[hh][c][:], in1=acch[c][:], op=ALU.subtract)
                        C.ve("pool", "tensor_copy", [("acc", hh, c)], [("accl", c)], out=accl[c][:], in_=acc[hh][c][:])
                        dbank = 4 + UC[0] % 4
                        UC[0] += 1
                        C.mm(C.banks[dbank][:, :], allones, acch[c][:], start=True, stop=False, reads=[("acch", c), "consts"],
                             writes=[("S", dbank)], bank=dbank)
                        C.mm(C.banks[dbank][:, :], allones, accl[c][:], start=False, stop=True, reads=[("accl", c), "consts"],
                             writes=[("S", dbank)], bank=dbank)
                        dbanks.append(dbank)
                    C.ve("dve", "reciprocal", [("S", dbanks[0])], ["r1"], banks=[dbanks[0]], out=r1[:], in_=C.banks[dbanks[0]][:, :])
                    C.ve("dve", "reciprocal", [("S", dbanks[1])], ["r2"], banks=[dbanks[1]], out=r2[:], in_=C.banks[dbanks[1]][:, :])
                    C.ve("dve", "tensor_tensor", [("O", ob0), "r1"], ["o1"], banks=[ob0], out=o1[:], in0=C.banks[ob0][:, :], in1=r1[:], op=ALU.mult)
                    C.ve("dve", "tensor_tensor", [("O", ob0 + 1), "r2"], ["o2"], banks=[ob0 + 1], out=o2[:], in0=C.banks[ob0 + 1][:, :], in1=r2[:], op=ALU.mult)
                    C.ve("dve", "scalar_tensor_tensor", ["o1", "o2", "lamn"], ["o"], out=o1[:], in0=o2[:], scalar=lamneg, in1=o1[:],
                         op0=ALU.mult, op1=ALU.add)
                    C.act(sqT[:], o1[:], AF.Square, reads=["o"], writes=["sqT"])
                    C.mm(C.banks[6][:, :], allones, sqT[:], start=True, stop=True, reads=["sqT", "consts"],
                         writes=[("pj", 6)], bank=6)
                    C.act(rstd[:], C.banks[6][:, :], AF.Ln, reads=[("pj", 6), "epsc"], writes=["rstd0"], banks=[6],
                          scale=1.0 / 128, bias=epsc[:])
                    C.act(rstd[:], rstd[:], AF.Exp, reads=["rstd0"], writes=["rstd"], scale=-0.5)
                    C.ve("dve", "scalar_tensor_tensor", ["o", "cols", "rstd"], ["og32"], out=o2[:], in0=o1[:], scalar=sublnS,
                         in1=rstd[:], op0=ALU.mult, op1=ALU.mult)
                    ob = ogt[hh]
                    C.ve("pool", "tensor_tensor", ["og32", ("GTt", hh)], [("ogt", hh)], out=ob[:], in0=o2[:], in1=GTt[:, hh, :], op=ALU.mult)
                    if F:
                        C.dma(ogTo[g // 8, hp * 2 + hh, :, (g % 8) * 512:(g % 8 + 1) * 512], ob[:], [("ogt", hh)], [("ogTo", hh)], eng="pool")
                    else:
                        C.dma(ogTo[hp * 2 + hh, :, g * 512:(g + 1) * 512], ob[:], [("ogt", hh)], [("ogTo", hh)], eng="pool")
        if F:
            def cc2():
                return nc.gpsimd.collective_compute("AllGather", ALU.bypass, replica_groups=[list(range(8))],
                                                    ins=[F["ogsrc"].ap().rearrange("j h e t -> (j h e) t")],
                                                    outs=[F["ogall"].ap()])
            C.P.op("pool", cc2, reads=[("ogTo", 0), ("ogTo", 1)], writes=["ogall"], dma=True, cc=True)
            C.barrier(["ogall"])
        else:
            C.P.op("sp", lambda: nc.sync.nop(), reads=[("ogTo", 0), ("ogTo", 1)])
        print("L1", C.P.emit())
    return nc


def build_l2(F=None):
    nc = F["nc"] if F else bass.Bass("TRN2", target_bir_lowering=False)
    dt = nc.dram_tensor
    ogT = None if F else dt("ogT", [8, 128, 4096], BF16, kind="ExternalInput").ap()
    x1 = F["x1s"].ap() if F else dt("x1", [4096, 1024], F32, kind="ExternalInput").ap()
    wo1 = dt("wo1", [1024, 1024], F32, kind="ExternalInput").ap()
    out = dt("out", [4096, 1024], F32, kind="ExternalOutput").ap()
    with ExitStack() as st:
        C = Ctx(nc, st, nf32=8, prefix="c_" if F else "", semstack=F["semstack"] if F else None, shared=F["shared"] if F else None)
        if F:
            fused_snap(C, F)
            oga3 = F["ogall"].ap().rearrange("(q n) t -> q n t", q=16)
            for r2 in range(2):
                for q2 in range(2):
                    def selo(r2=r2, q2=q2):
                        return nc.sync.dma_start(out=F["ogsel"].ap()[r2:r2 + 1, q2 * 256:(q2 + 1) * 256, :],
                                                 in_=oga3[bass.ds(F["vals"]["w%d" % r2], 1), q2 * 256:(q2 + 1) * 256, :])
                    C.dmaf(selo, [], [("ogsel", r2, q2)])
            ogsv = F["ogsel"].ap().rearrange("r (h e) t -> r e h t", h=4)
        sb = C.sb
        Wo = sb("Wo", [128, 8, 1024], BF16)
        stage = [sb("stg%d" % i, [128, 512], F32) for i in range(2)]
        og = [sb("og%d" % i, [128, 8, 512 if F else 128], BF16) for i in range(2)]
        xr = [sb("xr%d" % i, [128, 1024], F32) for i in range(2)]
        wokeys = load_weights(C, wo1, Wo, 1024, None, stage, "wo", 512)
        ogv = None if F else ogT.rearrange("c p t -> p c t")
        for ot in range(32):
            b = ot % 2
            if F:
                gq, gb, toff = ot // 4, (ot // 4) % 2, (ot % 4) * 128
                if ot % 4 == 0:
                    for r2 in range(2):
                        C.dma(og[gb][:, r2 * 4:(r2 + 1) * 4, :], ogsv[r2, :, :, gq * 512:(gq + 1) * 512],
                              [("ogsel", r2, 0), ("ogsel", r2, 1)], [("og", gb)] if r2 == 0 else [("og2", gb)])
                ogt_ = og[gb][:, :, toff:toff + 128]
                ogk = [("og", gb), ("og2", gb)]
            else:
                C.dma(og[b][:], ogv[:, :, ot * 128:(ot + 1) * 128], [], [("og", b)])
                ogt_ = og[b][:, :, :]
                ogk = [("og", b)]
            C.dma(xr[b][:], x1[ot * 128:(ot + 1) * 128, :], [], [("xr", b), ("o", b, 0), ("o", b, 1)])
            for half in range(2):
                bank = (ot % 2) * 2 + half
                for c in range(8):
                    C.mm(C.banks[bank][:, :], ogt_[:, c, :], Wo[:, c, half * 512:(half + 1) * 512],
                         start=(c == 0), stop=(c == 7), reads=ogk + wokeys, writes=[("pj", bank)], bank=bank)
                C.ve("dve", "tensor_tensor", [("pj", bank), ("xr", b)], [("o", b, half)], banks=[bank],
                     out=xr[b][:, half * 512:(half + 1) * 512], in0=C.banks[bank][:, :],
                     in1=xr[b][:, half * 512:(half + 1) * 512], op=ALU.add)
            C.dma(out[ot * 128:(ot + 1) * 128, :], xr[b][:], [("o", b, 0), ("o", b, 1)], [("out", b)], eng="pool")
        C.P.op("sp", lambda: nc.sync.nop(), reads=[("out", 0), ("out", 1)])
        print("L2", C.P.emit())
    return nc


def prep_l1(inp, h1T_full):
    win = np.asarray(inp["od_w_in"], np.float32)[0]
    qn = np.asarray(inp["od_q_norm"], np.float32)[0]
    kn = np.asarray(inp["od_k_norm"], np.float32)[0]
    odn = np.asarray(inp["od_norm"], np.float32)[0]
    subl = np.asarray(inp["od_subln"], np.float32)[0]
    p64 = np.arange(128) % 64
    p64s = (p64 + 32) % 64
    cols = np.zeros((128, 16), np.float32)
    cols[:, 0] = qn[p64]; cols[:, 1] = qn[p64s]; cols[:, 2] = kn[p64]; cols[:, 3] = kn[p64s]
    cols[:, 4:12] = odn.reshape(8, 128).T
    cols[:, 12] = subl
    lamv = np.stack([np.asarray(inp[k], np.float32)[0] for k in
                     ("od_lambda_q1", "od_lambda_k1", "od_lambda_q2", "od_lambda_k2")], axis=0)
    lamv = np.ascontiguousarray(np.broadcast_to(lamv[None], (128, 4, 64)))
    l = np.arange(128)[:, None]
    q = np.arange(512)[None, :]
    mC = np.zeros((128, 4, 512), np.float32)
    for j in range(4):
        mC[:, j, :] = np.where(q < 128 * j + np.where(l >= 64, 64, 0), NEG, 0.0)
    mC = mC.astype(NPBF)
    cst = np.zeros((128, 384), np.float32)
    cst[:, 0:256] = _consts().astype(np.float32)
    cst[:, 256:384] = 1.0
    cst = cst.astype(NPBF)
    rope = _rope_tables(np.arange(8192))
    maps = []
    for c in range(8):
        b, p = c // 2, c % 2
        idx = []
        for hp in range(2):
            H = [4 * p + 2 * hp, 4 * p + 2 * hp + 1]
            for base in (0, 1024):
                for h in H:
                    idx += list(range(base + h * 128, base + (h + 1) * 128))
            for base in (0, 1024):
                for h in H:
                    idx += _swap_idx(base + h * 128, 2)
            for h in H:
                idx += list(range(3072 + h * 128, 3072 + (h + 1) * 128))
            for h in H:
                idx += list(range(2048 + h * 128, 2048 + (h + 1) * 128))
        assert len(idx) == 3072
        m = dict(w1=np.ascontiguousarray(win[:, idx]), cols1=cols, lamv=lamv, rope1=rope, maskC=mC, cst1=cst)
        if h1T_full is not None:
            m["h1T"] = h1T_full[b]
        maps.append(m)
    return maps


def mini_out(F, tag):
    nc = F["nc"]
    out = nc.dram_tensor("out", [4096, 1024], F32, kind="ExternalOutput").ap()
    with ExitStack() as st:
        C = Ctx(nc, st, nf32=8, prefix=tag, semstack=F["semstack"], shared=F["shared"])
        for q in range(8):
            C.dma(out[q * 512:(q + 1) * 512, :], F["x1s"].ap()[q * 512:(q + 1) * 512, :], [], [("out", q)])
        C.P.op("sp", lambda: nc.sync.nop(), reads=[("out", q) for q in range(8)])
        print("mini", C.P.emit())


def build_fused(mode="ABC"):
    nc = bass.Bass("TRN2", target_bir_lowering=False)
    if "A" not in mode:
        return build_fused_noA(nc, mode)
    with ExitStack() as outer:
        F = dict(nc=nc, semstack=outer, vals={}, regs={}, shared={})
        F["idx"] = nc.dram_tensor("idx", [1, 4], mybir.dt.int32, kind="ExternalInput").ap()
        F["x1s"] = nc.dram_tensor("x1s", [4096, 1024], F32)
        F["h1src"] = nc.dram_tensor("h1src", [8, 128, 4096], BF16)
        F["h1all"] = nc.dram_tensor("h1all", [8 * 1024, 4096], BF16)
        F["ogsrc"] = nc.dram_tensor("ogsrc", [2, 4, 128, 4096], BF16)
        F["ogall"] = nc.dram_tensor("ogall", [8 * 1024, 4096], BF16)
        F["h1sel"] = nc.dram_tensor("h1sel", [2, 1024, 4096], BF16)
        F["ogsel"] = nc.dram_tensor("ogsel", [2, 512, 4096], BF16)
        for nm in ("v0", "v1", "w0", "w1"):
            F["regs"][nm] = outer.enter_context(nc.sync.register("r_" + nm))
        build_l0(F)
        if "B" in mode:
            build_l1(F)
        if "C" in mode:
            build_l2(F)
        else:
            mini_out(F, "m_")
    return nc


def build_fused_noA(nc, mode):
    with ExitStack() as outer:
        F = dict(nc=nc, semstack=outer, vals={}, regs={}, shared={})
        F["idx"] = nc.dram_tensor("idx", [1, 4], mybir.dt.int32, kind="ExternalInput").ap()
        F["x1s"] = nc.dram_tensor("x1s", [4096, 1024], F32)
        F["h1src"] = nc.dram_tensor("h1src", [8, 128, 4096], BF16)
        F["h1all"] = nc.dram_tensor("h1all", [8 * 1024, 4096], BF16)
        F["ogsrc"] = nc.dram_tensor("ogsrc", [2, 4, 128, 4096], BF16)
        F["ogall"] = nc.dram_tensor("ogall", [8 * 1024, 4096], BF16)
        F["h1sel"] = nc.dram_tensor("h1sel", [2, 1024, 4096], BF16)
        F["ogsel"] = nc.dram_tensor("ogsel", [2, 512, 4096], BF16)
        for nm in ("v0", "v1", "w0", "w1"):
            F["regs"][nm] = outer.enter_context(nc.sync.register("r_" + nm))
        with ExitStack() as st:
            C = Ctx(nc, st, nf32=8, prefix="z_", semstack=outer, shared=F["shared"])
            fused_load_regs(C, F)
            C.barrier(["idxt"])
            print("pre", C.P.emit())
        if "B" in mode:
            build_l1(F)
        if "C" in mode:
            build_l2(F)
        else:
            mini_out(F, "m_")
    return nc


FUSED = True
_NC_CACHE = {}


def _get(name, fn):
    if name not in _NC_CACHE:
        _NC_CACHE[name] = fn()
    return _NC_CACHE[name]


def kernel(**inputs):
    cores = list(range(8))
    if FUSED:
        maps0 = prep_l0(inputs)
        maps1 = prep_l1(inputs, None)
        wo1 = np.ascontiguousarray(np.asarray(inputs["od_w_out"], np.float32)[0])
        maps = []
        for c in cores:
            b, p = c // 2, c % 2
            m = dict(maps0[c])
            m.update(maps1[c])
            m["wo1"] = wo1
            m["idx"] = np.array([[2 * b, 2 * b + 1, 4 * b + p, 4 * b + 2 + p]], np.int32)
            maps.append(m)
        r = run_bass_kernel_spmd(_get("fused", build_fused), maps, core_ids=cores).results
        out = np.zeros((4, 8192, 1024), np.float32)
        for c in cores:
            b, p = c // 2, c % 2
            out[b, p * 4096:(p + 1) * 4096] = np.asarray(r[c]["out"])
        return out
    maps0 = prep_l0(inputs)
    r0 = run_bass_kernel_spmd(_get("l0", build_l0), maps0, core_ids=cores).results
    h1T_full = [np.ascontiguousarray(np.concatenate([np.asarray(r0[2 * b]["h1To"]), np.asarray(r0[2 * b + 1]["h1To"])], axis=2))
                for b in range(4)]
    maps1 = prep_l1(inputs, h1T_full)
    r1 = run_bass_kernel_spmd(_get("l1", build_l1), maps1, core_ids=cores).results
    wo1 = np.ascontiguousarray(np.asarray(inputs["od_w_out"], np.float32)[0])
    maps2 = []
    for c in cores:
        b, p = c // 2, c % 2
        og = np.concatenate([np.asarray(r1[2 * b]["ogTo"]), np.asarray(r1[2 * b + 1]["ogTo"])], axis=0)
        maps2.append(dict(ogT=np.ascontiguousarray(og[:, :, p * 4096:(p + 1) * 4096]),
                          x1=np.asarray(r0[c]["x1o"]), wo1=wo1))
    r2 = run_bass_kernel_spmd(_get("l2", build_l2), maps2, core_ids=cores).results
    out = np.zeros((4, 8192, 1024), np.float32)
    for c in cores:
        b, p = c // 2, c % 2
        out[b, p * 4096:(p + 1) * 4096] = np.asarray(r2[c]["out"])
    return out
```
